# Optimizing a Trainium2 kernel written in Bass

```python
import math
import numpy as np
import jax
import jax.numpy as jnp
from jax import lax

D_MODEL = 2048
BATCH = 4
SEQ = 2048
DEPTH = 4

F32 = jnp.float32
NORM_EPS = 1e-5

M_HEADS = 16
M_HEAD_DIM = 64
M_WIDTH = M_HEADS * M_HEAD_DIM
M_GROUPS = 4
M_STATE = 128
M_XBC = M_WIDTH + 2 * M_GROUPS * M_STATE
M_CONV = 4
M_CHUNK = 64

H_HEADS = 8
H_KEY = 128
H_VAL = 128
H_WIDTH = H_HEADS * H_VAL
H_CHUNK = 16

DN_HEADS = 8
DN_KEY = 128
DN_VAL = 128
DN_WIDTH = DN_HEADS * DN_VAL
DN_CONV = 4
DN_CHUNK = 64

G_HEADS = 4
G_KEY = 128
G_VAL = 256
G_WIDTH = G_HEADS * G_VAL
G_GATE_RANK = 16
G_GATE_TEMP = 16.0
G_CHUNK = 64

N_BRANCH = 4
BRANCH_WIDTH = 1024

IN_SIZES = (M_WIDTH, M_XBC, M_HEADS,
            H_HEADS * H_KEY, H_HEADS * H_KEY, H_WIDTH, H_WIDTH,
            DN_HEADS * DN_KEY, DN_HEADS * DN_KEY, DN_WIDTH, DN_HEADS, DN_HEADS, DN_WIDTH,
            G_HEADS * G_KEY, G_HEADS * G_KEY, G_WIDTH, G_WIDTH, G_GATE_RANK)
IN_WIDTH = sum(IN_SIZES)

P_HEADS = 8
P_NKEYS = 128
P_EXPERTS = P_NKEYS * P_NKEYS
P_QDIM = 256
P_HALF = P_QDIM // 2
P_TOPK = 16
P_TOKEN_BLOCK = 128

ALPHA = (2 * DEPTH) ** 0.25
BETA = (8 * DEPTH) ** -0.25

kernel_name = "hybrid_ssd_hgrn2_gdn_gla_peer"


def _rms(x):
    return x * lax.rsqrt(jnp.mean(x * x, axis=-1, keepdims=True) + NORM_EPS)


def layer_norm(x, gain, bias):
    xf = x.astype(F32)
    xc = xf - jnp.mean(xf, axis=-1, keepdims=True)
    y = xc * lax.rsqrt(jnp.mean(xc * xc, axis=-1, keepdims=True) + NORM_EPS)
    return (y * gain.astype(F32) + bias.astype(F32)).astype(x.dtype)


def causal_dwconv(x, w, b=None):
    k, c = w.shape
    y = lax.conv_general_dilated(x, w.astype(x.dtype)[:, None, :], window_strides=(1,),
                                 padding=[(k - 1, 0)], dimension_numbers=("NWC", "WIO", "NWC"),
                                 feature_group_count=c)
    if b is not None:
        y = y + b.astype(y.dtype)
    return y


def _masked_decay(diff, mask):
    return jnp.where(mask, jnp.exp(jnp.minimum(diff, 0.0)), 0.0)


def chunked_gla(q, k, v, log_g, chunk):
    bsz, s, nh, dk = q.shape
    dv = v.shape[-1]
    nc = s // chunk
    q, k, v, log_g = (t.reshape(bsz, nc, chunk, nh, t.shape[-1]) for t in (q, k, v, log_g))
    b = jnp.cumsum(log_g, axis=2)
    mid = chunk // 2
    b_mid = b[:, :, mid - 1:mid]
    causal = jnp.tril(jnp.ones((chunk, chunk), bool))
    att = jnp.where(causal, jnp.einsum("bclhk,bcshk->bchls", q * jnp.exp(b - b_mid),
                                       k * jnp.exp(b_mid - b)), 0.0)
    o_intra = jnp.einsum("bchls,bcshv->bclhv", att, v)
    q_dec = q * jnp.exp(b)
    k_end = k * jnp.exp(b[:, :, -1:] - b)
    kv = jnp.einsum("bclhk,bclhv->bchkv", k_end, v)
    g_end = jnp.exp(b[:, :, -1])

    def step(state, inp):
        kv_c, g_c = inp
        return g_c[..., None] * state + kv_c, state

    _, s_in = lax.scan(step, jnp.zeros((bsz, nh, dk, dv), F32),
                       (jnp.moveaxis(kv, 1, 0), jnp.moveaxis(g_end, 1, 0)))
    o_inter = jnp.einsum("bclhk,bchkv->bclhv", q_dec, jnp.moveaxis(s_in, 0, 1))
    return (o_intra + o_inter).reshape(bsz, s, nh, dv)


def ssd_branch(z, xbc, dt_raw, conv_w, conv_b, dt_bias, a_log, d_skip, norm_w):
    bsz, s, _ = z.shape
    L = M_CHUNK
    nc = s // L
    hg = M_HEADS // M_GROUPS
    xbc = jax.nn.silu(causal_dwconv(xbc, conv_w, conv_b)).astype(F32)
    xs, bm, cm = jnp.split(xbc, [M_WIDTH, M_WIDTH + M_GROUPS * M_STATE], axis=-1)
    x = xs.reshape(bsz, nc, L, M_GROUPS, hg, M_HEAD_DIM)
    bm = bm.reshape(bsz, nc, L, M_GROUPS, M_STATE)
    cm = cm.reshape(bsz, nc, L, M_GROUPS, M_STATE)
    dt = jax.nn.softplus(dt_raw.astype(F32) + dt_bias.astype(F32)).reshape(bsz, nc, L, M_GROUPS, hg)
    a = -jnp.exp(a_log.astype(F32)).reshape(M_GROUPS, hg)
    a_cs = jnp.cumsum(dt * a, axis=2)
    xdt = x * dt[..., None]
    causal = jnp.tril(jnp.ones((L, L), bool))[:, :, None, None]
    seg = _masked_decay(a_cs[:, :, :, None] - a_cs[:, :, None, :], causal)
    cb = jnp.einsum("bclgn,bcsgn->bclsg", cm, bm)
    y = jnp.einsum("bclsg,bclsgh,bcsghp->bclghp", cb, seg, xdt)
    states = jnp.einsum("bclgn,bclgh,bclghp->bcghpn", bm, jnp.exp(a_cs[:, :, -1:] - a_cs), xdt)

    def step(h, inp):
        st, dec = inp
        return dec[..., None, None] * h + st, h

    _, h_in = lax.scan(step, jnp.zeros((bsz, M_GROUPS, hg, M_HEAD_DIM, M_STATE), F32),
                       (jnp.moveaxis(states, 1, 0), jnp.moveaxis(jnp.exp(a_cs[:, :, -1]), 1, 0)))
    y = y + jnp.einsum("bclgn,bcghpn,bclgh->bclghp", cm, jnp.moveaxis(h_in, 0, 1), jnp.exp(a_cs))
    y = y + x * d_skip.astype(F32).reshape(M_GROUPS, hg)[:, :, None]
    y = y.reshape(bsz, s, M_WIDTH) * jax.nn.silu(z.astype(F32))
    y = _rms(y.reshape(bsz, s, M_GROUPS, M_WIDTH // M_GROUPS)).reshape(bsz, s, M_WIDTH)
    return (y * norm_w.astype(F32)).astype(z.dtype)


def hgrn2_branch(q, f, i, g, lower_bound, norm_w):
    bsz, s, _ = q.shape
    pre = f.astype(F32)
    lb = lower_bound.astype(F32)
    forget = lb + (1.0 - lb) * jax.nn.sigmoid(pre)
    log_f = jnp.log(forget)
    k = (1.0 - lb) * jax.nn.sigmoid(-pre)
    shp = (bsz, s, H_HEADS, H_KEY)
    o = chunked_gla(jax.nn.silu(q.astype(F32)).reshape(shp), k.reshape(shp),
                    i.astype(F32).reshape(bsz, s, H_HEADS, H_VAL), log_f.reshape(shp), H_CHUNK)
    o = _rms(o) * norm_w.astype(F32)
    return (o.reshape(bsz, s, H_WIDTH) * jax.nn.sigmoid(g.astype(F32))).astype(q.dtype)


def gdn_branch(q, k, v, a, b, g, conv_w, a_log, dt_bias, norm_w):
    bsz, s, _ = q.shape
    L = DN_CHUNK
    nc = s // L
    qkv = jax.nn.silu(causal_dwconv(jnp.concatenate([q, k, v], axis=-1), conv_w)).astype(F32)
    q, k, v = jnp.split(qkv, [DN_HEADS * DN_KEY, 2 * DN_HEADS * DN_KEY], axis=-1)

    def chunk(t, dim):
        return jnp.swapaxes(t.reshape(bsz, nc, L, DN_HEADS, dim), 2, 3)

    q, k, v = chunk(q, DN_KEY), chunk(k, DN_KEY), chunk(v, DN_VAL)
    q = q * lax.rsqrt(jnp.sum(q * q, axis=-1, keepdims=True) + 1e-6) * DN_KEY ** -0.5
    k = k * lax.rsqrt(jnp.sum(k * k, axis=-1, keepdims=True) + 1e-6)
    beta = jnp.swapaxes(jax.nn.sigmoid(b.astype(F32)).reshape(bsz, nc, L, DN_HEADS), 2, 3)
    log_alpha = -jnp.exp(a_log.astype(F32)) * jax.nn.softplus(a.astype(F32) + dt_bias.astype(F32))
    decay = jnp.cumsum(jnp.swapaxes(log_alpha.reshape(bsz, nc, L, DN_HEADS), 2, 3), axis=-1)
    incl = jnp.tril(jnp.ones((L, L), bool))
    strict = jnp.tril(jnp.ones((L, L), bool), -1)
    lmask = _masked_decay(decay[..., :, None] - decay[..., None, :], incl)
    k_beta = k * beta[..., None]
    a_mat = jnp.where(strict, jnp.einsum("bchlk,bchsk->bchls", k_beta, k) * lmask, 0.0)
    rhs = jnp.concatenate([k_beta * jnp.exp(decay)[..., None], v * beta[..., None]], axis=-1)
    sol = lax.linalg.triangular_solve(a_mat + jnp.eye(L, dtype=F32), rhs, left_side=True,
                                      lower=True, unit_diagonal=True)
    k_cum, u = jnp.split(sol, [DN_KEY], axis=-1)
    att = jnp.einsum("bchlk,bchsk->bchls", q, k) * lmask
    q_dec = q * jnp.exp(decay)[..., None]
    k_end = k * jnp.exp(decay[..., -1:] - decay)[..., None]
    g_end = jnp.exp(decay[..., -1])

    def step(state, inp):
        qd, kc, uc, ac, ke, ge = inp
        v_new = uc - jnp.einsum("bhlk,bhkv->bhlv", kc, state)
        o = jnp.einsum("bhlk,bhkv->bhlv", qd, state) + jnp.einsum("bhls,bhsv->bhlv", ac, v_new)
        state = ge[..., None, None] * state + jnp.einsum("bhlk,bhlv->bhkv", ke, v_new)
        return state, o

    xs = tuple(jnp.moveaxis(t, 1, 0) for t in (q_dec, k_cum, u, att, k_end, g_end))
    _, o = lax.scan(step, jnp.zeros((bsz, DN_HEADS, DN_KEY, DN_VAL), F32), xs)
    o = jnp.transpose(o, (1, 0, 3, 2, 4)).reshape(bsz, s, DN_HEADS, DN_VAL)
    o = _rms(o) * norm_w.astype(F32)
    return (o.reshape(bsz, s, DN_WIDTH) * jax.nn.silu(g.astype(F32))).astype(g.dtype)


def gla_branch(q, k, v, g, gate_lr, gate_w, gate_b, norm_w):
    bsz, s, _ = q.shape
    log_a = jax.nn.log_sigmoid(gate_lr.astype(F32) @ gate_w.astype(F32) + gate_b.astype(F32)) / G_GATE_TEMP
    shp = (bsz, s, G_HEADS, G_KEY)
    o = chunked_gla(q.astype(F32).reshape(shp) * G_KEY ** -0.5, k.astype(F32).reshape(shp),
                    v.astype(F32).reshape(bsz, s, G_HEADS, G_VAL), log_a.reshape(shp), G_CHUNK)
    o = _rms(o) * norm_w.astype(F32)
    return (o.reshape(bsz, s, G_WIDTH) * jax.nn.silu(g.astype(F32))).astype(g.dtype)


def peer_ffn(x, w_query, sub_keys, expert_u, expert_v):
    bsz, s, d = x.shape
    qr = (x @ w_query).astype(F32).reshape(bsz, s, P_HEADS, 2, P_HALF)
    scores = jnp.einsum("bshtc,tnc->bshtn", qr, sub_keys.astype(F32))
    top_v, top_i = lax.top_k(scores, P_TOPK)
    cand_v = (top_v[..., 0, :, None] + top_v[..., 1, None, :]).reshape(bsz, s, P_HEADS, P_TOPK * P_TOPK)
    cand_id = (top_i[..., 0, :, None] * P_NKEYS + top_i[..., 1, None, :]).reshape(bsz, s, P_HEADS, P_TOPK * P_TOPK)
    best_v, best_pos = lax.top_k(cand_v, P_TOPK)
    expert_id = jnp.take_along_axis(cand_id, best_pos, axis=-1)
    gate = jax.nn.softmax(best_v, axis=-1)
    nb = bsz * s // P_TOKEN_BLOCK
    xb = x.reshape(nb, P_TOKEN_BLOCK, d)
    idb = expert_id.reshape(nb, P_TOKEN_BLOCK, P_HEADS * P_TOPK)
    gb = gate.reshape(nb, P_TOKEN_BLOCK, P_HEADS * P_TOPK).astype(x.dtype)

    def block(args):
        xt, ids, gt = args
        u = expert_u[ids]
        vv = expert_v[ids]
        act = jax.nn.gelu(jnp.einsum("td,ted->te", xt, u))
        return jnp.einsum("te,ted->td", gt * act, vv)

    return lax.map(block, (xb, idb, gb)).reshape(bsz, s, d)


def _inv_softplus_dt(key, shape):
    dt = jnp.exp(jax.random.uniform(key, shape, F32, math.log(1e-3), math.log(1e-1)))
    return dt + jnp.log(-jnp.expm1(-dt))


def setup_inputs(seed: int = 0) -> dict:
    key = jax.random.key(seed)
    ks = jax.random.split(key, 32)

    def nrm(k, shape, scale):
        return jax.random.normal(k, shape, F32) * scale

    def gain(k, shape):
        return 1.0 + nrm(k, shape, 0.02)

    return {
        "x": nrm(ks[0], (BATCH, SEQ, D_MODEL), 1.0),
        "w_in": nrm(ks[1], (DEPTH, D_MODEL, IN_WIDTH), D_MODEL ** -0.5),
        "m_conv_w": nrm(ks[2], (DEPTH, M_CONV, M_XBC), M_CONV ** -0.5),
        "m_conv_b": nrm(ks[3], (DEPTH, M_XBC), 0.02),
        "m_dt_bias": _inv_softplus_dt(ks[4], (DEPTH, M_HEADS)),
        "m_a_log": jnp.log(jax.random.uniform(ks[5], (DEPTH, M_HEADS), F32, 1.0, 16.0)),
        "m_d_skip": gain(ks[6], (DEPTH, M_HEADS)),
        "m_norm_w": gain(ks[7], (DEPTH, M_WIDTH)),
        "h_lb_logits": nrm(ks[8], (DEPTH, H_HEADS * H_KEY), 0.1),
        "h_norm_w": gain(ks[9], (DEPTH, H_VAL)),
        "dn_conv_w": nrm(ks[10], (DEPTH, DN_CONV, 2 * DN_HEADS * DN_KEY + DN_WIDTH), DN_CONV ** -0.5),
        "dn_a_log": jnp.log(jax.random.uniform(ks[11], (DEPTH, DN_HEADS), F32, 1.0, 16.0)),
        "dn_dt_bias": _inv_softplus_dt(ks[12], (DEPTH, DN_HEADS)),
        "dn_norm_w": gain(ks[13], (DEPTH, DN_VAL)),
        "g_gate_w": nrm(ks[14], (DEPTH, G_GATE_RANK, G_HEADS * G_KEY), G_GATE_RANK ** -0.5),
        "g_gate_b": nrm(ks[15], (DEPTH, G_HEADS * G_KEY), 0.02),
        "g_norm_w": gain(ks[16], (DEPTH, G_VAL)),
        "w_branch": nrm(ks[17], (DEPTH, N_BRANCH, BRANCH_WIDTH, D_MODEL), BRANCH_WIDTH ** -0.5),
        "w_merge": nrm(ks[18], (DEPTH, D_MODEL, N_BRANCH * D_MODEL), D_MODEL ** -0.5),
        "b_merge": nrm(ks[19], (DEPTH, N_BRANCH * D_MODEL), 0.02),
        "w_out": nrm(ks[20], (DEPTH, D_MODEL, D_MODEL), BETA * D_MODEL ** -0.5),
        "ln1_w": gain(ks[21], (DEPTH, D_MODEL)),
        "ln1_b": nrm(ks[22], (DEPTH, D_MODEL), 0.02),
        "p_w_query": nrm(ks[23], (DEPTH, D_MODEL, P_HEADS * P_QDIM), D_MODEL ** -0.5),
        "p_sub_keys": nrm(ks[24], (DEPTH, 2, P_NKEYS, P_HALF), P_HALF ** -0.5),
        "p_expert_u": nrm(ks[25], (DEPTH, P_EXPERTS, D_MODEL), D_MODEL ** -0.5),
        "p_expert_v": nrm(ks[26], (DEPTH, P_EXPERTS, D_MODEL), BETA * P_HEADS ** -0.5),
        "ln2_w": gain(ks[27], (DEPTH, D_MODEL)),
        "ln2_b": nrm(ks[28], (DEPTH, D_MODEL), 0.02),
    }


def reference(x, w_in, m_conv_w, m_conv_b, m_dt_bias, m_a_log, m_d_skip, m_norm_w,
              h_lb_logits, h_norm_w, dn_conv_w, dn_a_log, dn_dt_bias, dn_norm_w,
              g_gate_w, g_gate_b, g_norm_w, w_branch, w_merge, b_merge, w_out, ln1_w, ln1_b,
              p_w_query, p_sub_keys, p_expert_u, p_expert_v, ln2_w, ln2_b):
    bsz, s, _ = x.shape
    split_at = np.cumsum(IN_SIZES)[:-1].tolist()
    lb_p = jax.nn.softmax(h_lb_logits.astype(F32), axis=0)
    lower_bounds = jnp.cumsum(lb_p, axis=0) - lb_p[0]
    for l in range(DEPTH):
        proj = x @ w_in[l]
        (m_z, m_xbc, m_dt, h_q, h_f, h_i, h_g, d_q, d_k, d_v, d_a, d_b, d_g,
         g_q, g_k, g_v, g_g, g_lr) = jnp.split(proj, split_at, axis=-1)
        y_m = ssd_branch(m_z, m_xbc, m_dt, m_conv_w[l], m_conv_b[l], m_dt_bias[l], m_a_log[l],
                         m_d_skip[l], m_norm_w[l])
        y_h = hgrn2_branch(h_q, h_f, h_i, h_g, lower_bounds[l], h_norm_w[l])
        y_d = gdn_branch(d_q, d_k, d_v, d_a, d_b, d_g, dn_conv_w[l], dn_a_log[l], dn_dt_bias[l],
                         dn_norm_w[l])
        y_g = gla_branch(g_q, g_k, g_v, g_g, g_lr, g_gate_w[l], g_gate_b[l], g_norm_w[l])
        branches = jnp.stack([y_m, y_h, y_d, y_g], axis=2)
        branch_d = jnp.einsum("bsnc,ncd->bsnd", branches, w_branch[l])
        gates = jax.nn.sigmoid((x @ w_merge[l] + b_merge[l]).astype(F32)).reshape(bsz, s, N_BRANCH, D_MODEL)
        merged = jnp.einsum("bsnd,bsnd->bsd", gates.astype(x.dtype), branch_d)
        x = layer_norm(ALPHA * x + merged @ w_out[l], ln1_w[l], ln1_b[l])
        ffn = peer_ffn(x, p_w_query[l], p_sub_keys[l], p_expert_u[l], p_expert_v[l])
        x = layer_norm(ALPHA * x + ffn, ln2_w[l], ln2_b[l])
    return x
```

```python
import numpy as np
from contextlib import ExitStack
import concourse.bass as bass
import concourse.mybir as mybir
from concourse.bass_utils import run_bass_kernel_spmd

F32 = mybir.dt.float32
BF16 = mybir.dt.bfloat16
I32 = mybir.dt.int32
U32 = mybir.dt.uint32
AF = mybir.ActivationFunctionType
ALU = mybir.AluOpType
AX = mybir.AxisListType

COMPUTE = ("pe", "act", "dve", "pool")
QUEUES = ("sp", "actq", "poolq")


class Buf:
    __slots__ = ("name", "lastw", "reads", "dsem", "dcnt", "pend_w", "pend_r", "is_dram", "uid")
    _next_uid = [0]

    def __init__(self, name, is_dram=False):
        Buf._next_uid[0] += 1
        self.uid = "b%d" % Buf._next_uid[0]
        self.name = name
        self.lastw = None
        self.reads = {}
        self.dsem = None
        self.dcnt = 0
        self.pend_w = {}
        self.pend_r = {}
        self.is_dram = is_dram


class T:
    def __init__(self, h, buf):
        self.h = h
        self.buf = buf

    def __getitem__(self, idx):
        return V(self.h[idx], self.buf)

    @property
    def ap(self):
        return V(self.h[:], self.buf)


class V:
    __slots__ = ("ap", "buf")

    def __init__(self, ap, buf):
        self.ap = ap
        self.buf = buf

    def __getitem__(self, idx):
        return V(self.ap[idx], self.buf)

    def rearrange(self, *a, **k):
        return V(self.ap.rearrange(*a, **k), self.buf)

    def bitcast(self, dt):
        return V(self.ap.bitcast(dt), self.buf)

    def to_broadcast(self, shape):
        return V(self.ap.to_broadcast(shape), self.buf)


def _ap(x):
    return x.ap if isinstance(x, V) else x


class Scope(ExitStack):
    def __init__(self, p):
        super().__init__()
        self.p = p
        self.bufs = []

    def __exit__(self, *a):
        if a[0] is None:
            self.p.barrier()
            for b in self.bufs:
                if b.dsem is not None:
                    self.p.free_dsems.append((b.dsem, b.dcnt))
                    if b in self.p.dma_bufs:
                        self.p.dma_bufs.remove(b)
                    b.dsem = None
        return super().__exit__(*a)


class Prog:
    def __init__(self, nc, es):
        self.nc = nc
        self.es = es
        self.allsems = []
        self.dma_bufs = []
        self.free_dsems = []
        self.sem = {}
        self.cnt = {}
        self.seen = {}
        self.ops = {e: [] for e in ("pe", "act", "dve", "pool", "sp")}
        for e in COMPUTE:
            self.sem[e] = self.newsem("s_" + e)
            self.cnt[e] = 0
        self.semkeys = {}
        for s in ("pe", "act", "dve", "pool", "sp"):
            self.seen[s] = {}
        self.nbuf = 0
        self.nalloc = 0
        self.n_inst = 0
        self.final_waits = []

    def scope(self):
        return Scope(self)

    def barrier(self):
        for stream in ("pe", "act", "dve", "pool", "sp"):
            waits = []
            for e in COMPUTE:
                if e == stream:
                    continue
                seen = self.seen[stream]
                if seen.get(e, 0) < self.cnt[e]:
                    seen[e] = self.cnt[e]
                    waits.append((self.sem[e], self.cnt[e]))
            for b in self.dma_bufs:
                if b.dsem is not None and b.dcnt:
                    self._need(stream, b.uid, b.dsem, 16 * b.dcnt, waits)
            if waits:
                self.ops[stream].append((waits, None, []))

    def newsem(self, name):
        h = self.nc.alloc_semaphore(name=name)
        self.allsems.append(h)
        return h

    def sb(self, name, shape, dt, stack=None):
        self.nalloc += 1
        h = (stack or self.es).enter_context(self.nc.sbuf_tensor("%s_%d" % (name, self.nalloc), list(shape), dt))
        b = Buf(name)
        if isinstance(stack, Scope):
            stack.bufs.append(b)
        return T(h, b)

    def ps(self, name, shape, dt=F32, stack=None):
        self.nalloc += 1
        h = (stack or self.es).enter_context(self.nc.psum_tensor("%s_%d" % (name, self.nalloc), list(shape), dt))
        b = Buf(name)
        if isinstance(stack, Scope):
            stack.bufs.append(b)
        return T(h, b)

    def dram(self, name, shape, dt, kind="Internal"):
        h = self.nc.dram_tensor(name, list(shape), dt, kind=kind)
        return T(h.ap(), Buf(name, is_dram=True))

    def split(self, t, name):
        return T(t.h, Buf(name, is_dram=t.buf.is_dram))

    def _stream(self, eng):
        return {"pe": "pe", "act": "act", "dve": "dve", "pool": "pool", "sp": "sp",
                "actq": "act", "poolq": "pool"}[eng]

    def _need(self, stream, sem_key, sem, val, waits):
        if val <= 0:
            return
        if stream == "pe" and sem_key == "pe":
            return
        seen = self.seen[stream]
        if seen.get(sem_key, 0) >= val:
            return
        seen[sem_key] = val
        waits.append((sem, val))

    def _deps_compute(self, stream, reads, writes, waits):
        for b in reads:
            if b.lastw is not None:
                e, n = b.lastw
                self._need(stream, e, self.sem[e], n, waits)
            if b.dsem is not None and b.dcnt:
                self._need(stream, b.uid, b.dsem, 16 * b.dcnt, waits)
        for b in writes:
            if b.lastw is not None:
                e, n = b.lastw
                self._need(stream, e, self.sem[e], n, waits)
            for e, n in b.reads.items():
                self._need(stream, e, self.sem[e], n, waits)
            if b.dsem is not None and b.dcnt:
                self._need(stream, b.uid, b.dsem, 16 * b.dcnt, waits)

    def op(self, eng, fn, reads=(), writes=()):
        assert eng in COMPUTE
        rb = [r.buf for r in reads if r is not None and hasattr(r, "buf")]
        wb = [w.buf for w in writes if w is not None and hasattr(w, "buf")]
        waits = []
        self._deps_compute(eng, rb, wb, waits)
        self.cnt[eng] += 1
        n = self.cnt[eng]
        sem = self.sem[eng]
        self.ops[eng].append((waits, fn, [(sem, 1)]))
        for b in wb:
            b.lastw = (eng, n)
            b.reads = {}
        for b in rb:
            if b not in wb:
                b.reads[eng] = n
        self.n_inst += 1

    def dma(self, q, out, in_, **kw):
        stream = self._stream(q)
        waits = []
        ob, ib = out.buf, in_.buf
        sb_side = None
        for b, is_w in ((ib, False), (ob, True)):
            if b.is_dram:
                for k, (s_, v_) in b.pend_w.items():
                    self._need(stream, k, s_, v_, waits)
                if is_w:
                    for k, (s_, v_) in b.pend_r.items():
                        self._need(stream, k, s_, v_, waits)
            else:
                sb_side = b if sb_side is None else sb_side
                if b.lastw is not None:
                    e, n = b.lastw
                    self._need(stream, e, self.sem[e], n, waits)
                if is_w:
                    for e, n in b.reads.items():
                        self._need(stream, e, self.sem[e], n, waits)
                if b.dsem is not None and b.dcnt:
                    self._need(stream, b.uid, b.dsem, 16 * b.dcnt, waits)
        assert sb_side is not None, "DRAM->DRAM DMA not supported here"
        if sb_side.dsem is None:
            if self.free_dsems:
                sb_side.dsem, sb_side.dcnt = self.free_dsems.pop()
            else:
                sb_side.dsem = self.newsem("d%d" % len(self.allsems))
            self.dma_bufs.append(sb_side)
        sb_side.dcnt += 1
        val = 16 * sb_side.dcnt
        sem = sb_side.dsem
        if ob.is_dram:
            ob.pend_w[sb_side.uid] = (sem, val)
        if ib.is_dram:
            ib.pend_r[sb_side.uid] = (sem, val)
        oap, iap = out.ap, in_.ap

        def fn(eng, oap=oap, iap=iap, kw=kw):
            return eng.dma_start(out=oap, in_=iap, **kw)

        self.ops[stream].append((waits, fn, [(sem, 16)]))
        if stream in COMPUTE:
            pass
        self.n_inst += 1
        return sem, val


    def mm(self, out, lhsT, rhs, start=True, stop=True):
        self.op("pe", lambda e, o=out.ap, l=lhsT.ap, r=rhs.ap: e.matmul(o, l, r, start=start, stop=stop),
                reads=[lhsT, rhs], writes=[out])

    def tr(self, out, in_, ident):
        self.op("pe", lambda e, o=out.ap, i=in_.ap, d=ident.ap: e.transpose(o, i, d), reads=[in_, ident], writes=[out])

    def tt(self, eng, out, in0, in1, op):
        self.op(eng, lambda e, o=out.ap, a=in0.ap, b=in1.ap: e.tensor_tensor(o, a, b, op), reads=[in0, in1], writes=[out])

    def ts(self, eng, out, in0, s1, s2=None, op0=ALU.mult, op1=None):
        rd = [in0] + [x for x in (s1, s2) if isinstance(x, V)]
        a1, a2 = _ap(s1), _ap(s2)
        if op1 is None:
            self.op(eng, lambda e, o=out.ap, a=in0.ap: e.tensor_scalar(o, a, a1, None, op0), reads=rd, writes=[out])
        else:
            self.op(eng, lambda e, o=out.ap, a=in0.ap: e.tensor_scalar(o, a, a1, a2, op0, op1), reads=rd, writes=[out])

    def stt(self, out, in0, scalar, in1, op0, op1):
        rd = [in0, in1] + ([scalar] if isinstance(scalar, V) else [])
        sc = _ap(scalar)
        self.op("dve", lambda e, o=out.ap, a=in0.ap, b=in1.ap: e.scalar_tensor_tensor(o, a, sc, b, op0, op1), reads=rd, writes=[out])

    def act(self, out, in_, func, bias=None, scale=None, accum_out=None):
        rd = [in_] + [x for x in (bias, scale) if isinstance(x, V)]
        wr = [out] + ([accum_out] if accum_out is not None else [])
        kw = {}
        if bias is not None:
            kw["bias"] = _ap(bias)
        if scale is not None:
            kw["scale"] = _ap(scale)
        if accum_out is not None:
            kw["accum_out"] = accum_out.ap
        self.op("act", lambda e, o=out.ap, i=in_.ap: e.activation(o, i, func, **kw), reads=rd, writes=wr)

    def copy(self, eng, out, in_):
        if eng == "act":
            self.op("act", lambda e, o=out.ap, i=in_.ap: e.copy(o, i), reads=[in_], writes=[out])
        else:
            self.op(eng, lambda e, o=out.ap, i=in_.ap: e.tensor_copy(o, i), reads=[in_], writes=[out])

    def memset(self, eng, out, val):
        self.op(eng, lambda e, o=out.ap: e.memset(o, val), reads=[], writes=[out])

    def scan(self, out, d0, d1, init, op0, op1):
        self.op("dve", lambda e, o=out.ap, a=d0.ap, b=d1.ap: e.tensor_tensor_scan(o, a, b, init, op0, op1), reads=[d0, d1], writes=[out])

    def recip(self, out, in_):
        self.op("dve", lambda e, o=out.ap, i=in_.ap: e.reciprocal(o, i), reads=[in_], writes=[out])

    def emit(self):
        nc = self.nc
        engobj = {"pe": "tensor", "act": "scalar", "dve": "vector", "pool": "gpsimd", "sp": "sync"}
        nc.clear_and_free_semaphores(list(self.allsems))
        nc.all_engine_barrier()
        with nc.Block() as block:
            for s, attr in engobj.items():
                ops = self.ops[s]
                fin = self.final_waits if s == "sp" else []

                def body(eng, ops=ops, fin=fin):
                    for waits, fn, incs in ops:
                        for (sem, val) in waits:
                            eng.wait_ge(sem, val)
                        if fn is None:
                            continue
                        ins = fn(eng)
                        for (sem, v) in incs:
                            ins = ins.then_inc(sem, v)
                    for (sem, val) in fin:
                        eng.wait_ge(sem, val)

                getattr(block, attr)(body)

    def finish_output(self, sem, val):
        self.final_waits.append((sem, val))


D = 2048
S = 2048
NT = S // 128
KC = D // 128
DEPTH = 4
IN_SIZES = (1024, 2048, 16, 1024, 1024, 1024, 1024, 1024, 1024, 1024, 8, 8, 1024, 512, 512, 1024, 1024, 16)
IN_NAMES = ("m_z", "m_xbc", "m_dt", "h_q", "h_f", "h_i", "h_g", "d_q", "d_k", "d_v", "d_a", "d_b", "d_g",
            "g_q", "g_k", "g_v", "g_g", "g_lr")
IN_OFF = dict(zip(IN_NAMES, np.concatenate([[0], np.cumsum(IN_SIZES)[:-1]]).tolist()))
IN_SZ = dict(zip(IN_NAMES, IN_SIZES))
FM_ORDER = ("m_xbc", "h_q", "h_f", "d_q", "d_k", "d_v", "g_q", "g_k", "g_lr")
TM_ORDER = ("m_z", "h_i", "h_g", "d_g", "g_v", "g_g", "m_dt", "d_a", "d_b")
FM_OFF, TM_OFF = {}, {}
_o = 0
for _n in FM_ORDER:
    FM_OFF[_n] = _o
    _o += IN_SZ[_n]
FM_W = _o
_o = 0
for _n in TM_ORDER:
    TM_OFF[_n] = _o
    _o += IN_SZ[_n]
TM_W = _o
W_IN_PERM = np.concatenate([np.arange(IN_OFF[n], IN_OFF[n] + IN_SZ[n]) for n in FM_ORDER + TM_ORDER])


def phase_xT(p, x_tm, xT, ident):
    ident = V(ident.ap, ident.buf) if isinstance(ident, V) else ident[:, :]
    with p.scope() as st:
        xin = [p.sb("xin%d" % i, [128, D], F32, st) for i in range(2)]
        pst = [p.ps("pst%d" % i, [128, 4, 128], F32, st) for i in range(2)]
        k = 0
        for t in range(NT):
            xi = xin[t % 2]
            p.dma("sp", xi[:, :], x_tm[t * 128:(t + 1) * 128, :])
            for g in range(KC // 4):
                pt = pst[k % 2]
                for j in range(4):
                    c = g * 4 + j
                    p.op("pe", lambda e, o=pt[:, j, :].ap, i=xi[:, c * 128:(c + 1) * 128].ap, idn=ident.ap:
                         e.transpose(o, i, idn), reads=[xi, ident], writes=[pt])
                dst = xT[:, g * 4:(g + 1) * 4, t * 128:(t + 1) * 128]
                eng = "dve" if k % 2 == 0 else "act"
                if eng == "dve":
                    p.op("dve", lambda e, o=dst.ap, i=pt[:, :, :].ap: e.tensor_copy(o, i), reads=[pt], writes=[xT])
                else:
                    p.op("act", lambda e, o=dst.ap, i=pt[:, :, :].ap: e.copy(o, i), reads=[pt], writes=[xT])
                k += 1


def phase_proj(p, xT, w_l, fm_s, tm_s):
    with p.scope() as st:
        wb = [p.sb("wb%d" % i, [128, KC, 512], BF16, st) for i in range(2)]
        wf = [p.sb("wf%d" % i, [128, KC, 512], F32, st) for i in range(2)]
        stg = [p.sb("stg%d" % i, [128, 2048], F32, st) for i in range(2)]
        psb = [p.ps("pp%d" % i, [128, 512], F32, st) for i in range(4)]
        ncol = FM_W + TM_W
        nsup = (ncol + 511) // 512
        ki = 0
        si = 0
        col = 0
        wi = 0
        while col < ncol:
            in_fm = col < FM_W
            lim = FM_W if in_fm else ncol
            cw = min(512, lim - col)
            w = wb[wi % 2]
            wf_ = wf[wi % 2]
            wi += 1
            p.dma("actq", wf_[:, :, :cw], w_l[:, col:col + cw].rearrange("(k p) c -> p k c", p=128))
            p.op("pool", lambda e, o=w[:, :, :cw].ap, i=wf_[:, :, :cw].ap: e.tensor_copy(o, i), reads=[wf_], writes=[w])
            if in_fm:
                for c0 in range(0, cw, 128):
                    m = min(128, cw - c0)
                    sg = stg[si % 2]
                    si += 1
                    for tt in range(4):
                        ps = psb[ki % 4]
                        ki += 1
                        for k in range(KC):
                            p.op("pe", lambda e, o=ps[:m, :].ap, l=w[:, k, c0:c0 + m].ap, r=xT[:, k, tt * 512:(tt + 1) * 512].ap, k=k:
                                 e.matmul(o, l, r, start=(k == 0), stop=(k == KC - 1)), reads=[w, xT], writes=[ps])
                        if ki % 2 == 0:
                            p.op("dve", lambda e, o=sg[:m, tt * 512:(tt + 1) * 512].ap, i=ps[:m, :].ap: e.tensor_copy(o, i), reads=[ps], writes=[sg])
                        else:
                            p.op("act", lambda e, o=sg[:m, tt * 512:(tt + 1) * 512].ap, i=ps[:m, :].ap: e.copy(o, i), reads=[ps], writes=[sg])
                    p.dma("sp", fm_s[col + c0:col + c0 + m, :], sg[:m, :])
            else:
                tcol = col - FM_W
                for t4 in range(NT // 4):
                    sg = stg[si % 2]
                    si += 1
                    for j in range(4):
                        t = t4 * 4 + j
                        ps = psb[ki % 4]
                        ki += 1
                        for k in range(KC):
                            p.op("pe", lambda e, o=ps[:, :cw].ap, l=xT[:, k, t * 128:(t + 1) * 128].ap, r=w[:, k, :cw].ap, k=k:
                                 e.matmul(o, l, r, start=(k == 0), stop=(k == KC - 1)), reads=[w, xT], writes=[ps])
                        if ki % 2 == 0:
                            p.op("dve", lambda e, o=sg[:, j * 512:j * 512 + cw].ap, i=ps[:, :cw].ap: e.tensor_copy(o, i), reads=[ps], writes=[sg])
                        else:
                            p.op("act", lambda e, o=sg[:, j * 512:j * 512 + cw].ap, i=ps[:, :cw].ap: e.copy(o, i), reads=[ps], writes=[sg])
                    for j in range(4):
                        t = t4 * 4 + j
                        p.dma("sp", tm_s[t * 128:(t + 1) * 128, tcol:tcol + cw], sg[:, j * 512:j * 512 + cw])
            col += cw


DBG = 0
LC = 16
NCH = 128 // LC


def host_consts():
    i = np.arange(128)
    same = (i[:, None] // LC) == (i[None, :] // LC)
    c = {}
    c["ident"] = np.eye(128, dtype=np.float32)
    c["maskBD"] = (same & (i[None, :] >= i[:, None])).astype(np.float32)
    c["negstrict"] = -(same & (i[:, None] > i[None, :])).astype(np.float32)
    c["reset"] = np.broadcast_to((i % LC != 0).astype(np.float32)[None, :], (128, 128)).copy()
    mc = np.zeros((128, NCH, 128), np.float32)
    mp = np.zeros((128, NCH, 128), np.float32)
    for ch in range(NCH):
        mc[:, ch, ch * LC:(ch + 1) * LC] = 1.0
        mp[ch * LC:(ch + 1) * LC, ch, :] = 1.0
    c["maskC"] = mc.reshape(128, NCH * 128)
    c["maskP"] = mp.reshape(128, NCH * 128)
    c["ones"] = np.ones((128, 128), np.float32)
    return c


CONST_ORDER = ("ident", "maskBD", "negstrict", "reset", "maskC", "maskP", "ones")


def host_const_array():
    c = host_consts()
    return np.ascontiguousarray(np.concatenate([c[k] for k in CONST_ORDER], axis=1))


class Consts:
    def __init__(self, p, cdram):
        c = host_consts()
        tot = sum(c[k].shape[1] for k in CONST_ORDER)
        self.f = p.sb("constf", [128, tot], F32)
        self.b = p.sb("constb", [128, tot], BF16)
        p.dma("sp", self.f[:, :], cdram[:, :])
        p.copy("dve", self.b[:, :], self.f[:, :])
        o = 0
        self.off = {}
        for k in CONST_ORDER:
            self.off[k] = (o, c[k].shape[1])
            o += c[k].shape[1]

    def F(self, k):
        o, w = self.off[k]
        return self.f[:, o:o + w]

    def B(self, k):
        o, w = self.off[k]
        return self.b[:, o:o + w]


def view(v, name):
    return V(v.ap, v.buf)


class MixRes:
    def __init__(self, p, st, C):
        self.C = C
        self.ps = [p.ps("mps%d" % i, [128, 512], F32, st) for i in range(8)]
        P = self.ps
        self.slots = []
        for s in range(2):
            d = {}
            d["attT"] = view(P[0][:, s * 128:(s + 1) * 128], "attT%d" % s)
            d["A"] = view(P[0][:, 256 + s * 128:256 + (s + 1) * 128], "Aps%d" % s)
            pb = P[1][:, :].bitcast(BF16)
            d["kendT"] = view(pb[:, s * 512:s * 512 + 128], "kendT%d" % s)
            d["kdecT"] = view(pb[:, s * 512 + 128:s * 512 + 256], "kdecT%d" % s)
            d["kcumT"] = view(pb[:, s * 512 + 256:s * 512 + 384], "kcumT%d" % s)
            d["yT"] = view(pb[:, s * 512 + 384:s * 512 + 512], "yT%d" % s)
            d["kv"] = [view(P[2 + s][:, j * 256:(j + 1) * 256], "kv%d_%d" % (s, j)) for j in range(2)]
            d["o"] = view(P[4][:, s * 256:(s + 1) * 256], "ops%d" % s)
            d["pc"] = view(P[5][:, s * 256:s * 256 + 128], "pc%d" % s)
            d["ft"] = view(P[5][:, s * 256 + 128:(s + 1) * 256], "ft%d" % s)
            d["sv"] = [view(P[6 + s][:, j * 256:(j + 1) * 256], "sv%d_%d" % (s, j)) for j in range(2)]
            for nm, shp, dt in (("b", [128, 128], F32), ("eb", [128, 128], F32), ("enb", [128, 128], F32),
                                ("qdec", [128, 128], BF16), ("kt", [128, 128], BF16), ("kend", [128, 128], BF16),
                                ("attm", [128, 128], BF16), ("qdm", [128, NCH * 128], BF16),
                                ("kem", [128, NCH * 128], BF16)):
                d[nm] = p.sb("mx_%s%d" % (nm, s), shp, dt, st)
            self.slots.append(d)
        self.n = 0

    def next(self):
        self.n += 1
        return self.slots[self.n % 2]


def mix_core(p, R, sl, q, k, logg, v_bf, Vd, S, S_bf, gdn=None):
    C = R.C
    b, eb, enb, qdec, kt, kend, attm, qdm, kem = (sl[x] for x in ("b", "eb", "enb", "qdec", "kt", "kend", "attm", "qdm", "kem"))
    if DBG == 1:
        return None
    p.scan(b[:, :], C.F("reset"), logg, 0.0, ALU.mult, ALU.add)
    p.act(eb[:, :], b[:, :], AF.Exp)
    p.act(enb[:, :], b[:, :], AF.Exp, scale=-1.0)
    p.tt("dve", qdec[:, :], q, eb[:, :], ALU.mult)
    p.tt("pool", kt[:, :], k, enb[:, :], ALU.mult)
    eb3 = eb[:, :].rearrange("p (c l) -> p c l", l=LC)
    gend_bc = eb3[:, :, LC - 1:LC].to_broadcast([128, NCH, LC])
    p.tt("dve", kend[:, :].rearrange("p (c l) -> p c l", l=LC), kt[:, :].rearrange("p (c l) -> p c l", l=LC), gend_bc, ALU.mult)
    if DBG == 2:
        return None
    p.mm(sl["attT"], kt[:, :], qdec[:, :])
    p.tt("dve", attm[:, :], sl["attT"], C.F("maskBD"), ALU.mult)
    qd_bc = qdec[:, :].rearrange("p (o l) -> p o l", o=1).to_broadcast([128, NCH, 128])
    p.tt("pool", qdm[:, :].rearrange("p (c l) -> p c l", l=128), qd_bc, C.B("maskC").rearrange("p (c l) -> p c l", l=128), ALU.mult)
    p.tr(sl["kendT"], kend[:, :], C.B("ident"))
    ke_bc = sl["kendT"].rearrange("p (o l) -> p o l", o=1).to_broadcast([128, NCH, 128])
    p.tt("dve", kem[:, :].rearrange("p (c l) -> p c l", l=128), ke_bc, C.B("maskP").rearrange("p (c l) -> p c l", l=128), ALU.mult)
    if DBG == 3:
        return None
    if gdn is not None:
        G = gdn
        beta, vf = G["beta"], G["vf"]
        kdec, Nm, NT, N2, N2T, N4, N4T, N8T, X, kcb = (G[x] for x in ("kdec", "N", "NT", "N2", "N2T", "N4", "N4T", "N8T", "X", "kcb"))
        sv = sl["sv"]
        p.tt("dve", kdec[:, :], k, eb[:, :], ALU.mult)
        if DBG == 61:
            return None
        Aps = sl["pc"]
        p.mm(Aps, kdec[:, :], kt[:, :])
        if DBG == 62:
            return None
        p.ts("dve", Nm[:, :], Aps, beta, None, ALU.mult)
        p.tt("dve", Nm[:, :], Nm[:, :], C.F("negstrict"), ALU.mult)
        if DBG == 6:
            return None
        p.tr(sl["ft"], Nm[:, :], C.F("ident"))
        p.copy("act", NT[:, :], sl["ft"])
        p.tr(sl["kdecT"], kdec[:, :], C.B("ident"))
        p.ts("dve", X[0][:, 0:128], sl["kdecT"], beta, None, ALU.mult)
        p.ts("pool", X[0][:, 128:256], vf, beta, None, ALU.mult)
        p.mm(sv[1], NT[:, :], X[0][:, :])
        p.tt("dve", X[1][:, :], X[0][:, :], sv[1], ALU.add)
        p.mm(sv[0][:, :128], NT[:, :], Nm[:, :])
        p.copy("act", N2[:, :], sv[0][:, :128])
        p.mm(sv[0][:, 128:256], Nm[:, :], NT[:, :])
        p.copy("act", N2T[:, :], sv[0][:, 128:256])
        p.mm(sv[1], N2T[:, :], X[1][:, :])
        p.tt("dve", X[0][:, :], X[1][:, :], sv[1], ALU.add)
        p.mm(sv[0][:, :128], N2T[:, :], N2[:, :])
        p.copy("act", N4[:, :], sv[0][:, :128])
        p.mm(sv[0][:, 128:256], N2[:, :], N2T[:, :])
        p.copy("act", N4T[:, :], sv[0][:, 128:256])
        p.mm(sv[1], N4T[:, :], X[0][:, :])
        p.tt("dve", X[1][:, :], X[0][:, :], sv[1], ALU.add)
        p.mm(sv[0][:, :128], N4[:, :], N4T[:, :])
        p.copy("act", N8T[:, :], sv[0][:, :128])
        p.mm(sv[1], N8T[:, :], X[1][:, :])
        p.tt("dve", X[0][:, :], X[1][:, :], sv[1], ALU.add)
        if DBG == 7:
            return None
        p.copy("pool", kcb[:, :], X[0][:, 0:128])
        p.tr(sl["kcumT"], kcb[:, :], C.B("ident"))
        kc_bc = sl["kcumT"].rearrange("p (o l) -> p o l", o=1).to_broadcast([128, NCH, 128])
        p.tt("dve", G["kcm"][:, :].rearrange("p (c l) -> p c l", l=128), kc_bc, C.B("maskC").rearrange("p (c l) -> p c l", l=128), ALU.mult)
        at_bc = attm[:, :].rearrange("p (o l) -> p o l", o=1).to_broadcast([128, NCH, 128])
        p.tt("pool", G["atm"][:, :].rearrange("p (c l) -> p c l", l=128), at_bc, C.B("maskP").rearrange("p (c l) -> p c l", l=128), ALU.mult)
        gdn = {"kcm": G["kcm"], "atm": G["atm"], "vn": G["vn"], "u": X[0][:, 128:256]}
    o = sl["o"][:, :Vd]
    for c in range(NCH):
        kv = sl["kv"][c % 2][:, :Vd]
        Sin = S_bf[:, c * Vd:(c + 1) * Vd]
        if gdn is not None:
            pc = sl["pc"]
            p.mm(pc, gdn["kcm"][:, c * 128:(c + 1) * 128], Sin)
            vsrc = gdn["vn"][:, c * Vd:(c + 1) * Vd]
            p.tt("dve", vsrc, gdn["u"], pc, ALU.subtract)
        else:
            vsrc = v_bf
        p.mm(kv, kem[:, c * 128:(c + 1) * 128], vsrc)
        p.stt(S[:, :], S[:, :], eb[:, c * LC + LC - 1:c * LC + LC], kv, ALU.mult, ALU.add)
        p.copy("act", S_bf[:, (c + 1) * Vd:(c + 2) * Vd], S[:, :])
    for c in range(NCH):
        p.mm(o, qdm[:, c * 128:(c + 1) * 128], S_bf[:, c * Vd:(c + 1) * Vd], start=(c == 0), stop=False)
        if gdn is not None:
            p.mm(o, gdn["atm"][:, c * 128:(c + 1) * 128], gdn["vn"][:, c * Vd:(c + 1) * Vd], start=False, stop=(c == NCH - 1))
    if gdn is None:
        p.mm(o, attm[:, :], v_bf, start=False, stop=True)
    p.copy("pool", S_bf[:, 0:Vd], S_bf[:, NCH * Vd:(NCH + 1) * Vd])
    return o


def bc_rows(v, n=128):
    return V(v.ap.to_broadcast([n, v.ap.shape[-1]]), v.buf)


def rms_gate_out(p, sl, o, Vd, gate_act, nw_bc, rstd_scr, y_bf, eps=1e-5):
    raise NotImplementedError


def rstd_from_psum(p, o, Vd, scr, eps=1e-5):
    junk = scr[:, :Vd]
    ss = scr[:, Vd:Vd + 1]
    r1 = scr[:, Vd + 1:Vd + 2]
    r2 = scr[:, Vd + 2:Vd + 3]
    p.act(junk, o, AF.Square, accum_out=ss)
    p.ts("dve", r1, ss, 1.0 / Vd, eps, ALU.mult, ALU.add)
    p.act(r1, r1, AF.Sqrt)
    p.recip(r2, r1)
    return r2


def phase_hgrn2(p, R, fm_s, tm_s, yT_s, lb, oml, nw_dram):
    H = 8
    with p.scope() as st:
        qf = [p.sb("h_qf%d" % i, [128, H, 128], F32, st) for i in range(2)]
        ff = [p.sb("h_ff%d" % i, [128, H, 128], F32, st) for i in range(2)]
        vt = [p.sb("h_vt%d" % i, [128, H * 128], F32, st) for i in range(2)]
        gt = [p.sb("h_gt%d" % i, [128, H * 128], F32, st) for i in range(2)]
        vb = [p.sb("h_vb%d" % i, [128, H * 128], BF16, st) for i in range(2)]
        sg = [p.sb("h_sg%d" % i, [128, H, 128], F32, st) for i in range(2)]
        lg = [p.sb("h_lg%d" % i, [128, 128], F32, st) for i in range(2)]
        kk = [p.sb("h_kk%d" % i, [128, 128], F32, st) for i in range(2)]
        gw = [p.sb("h_gw%d" % i, [128, H, 128], F32, st) for i in range(2)]
        scr = [p.sb("h_scr%d" % i, [128, 132], F32, st) for i in range(2)]
        yb = [p.sb("h_yb%d" % i, [128, 128], BF16, st) for i in range(2)]
        yT = [p.sb("h_yT%d" % i, [128, H, 128], BF16, st) for i in range(2)]
        nw = p.sb("h_nw", [128, 128], F32, st)
        Ss = [p.sb("h_S%d" % h, [128, 128], F32, st) for h in range(H)]
        Sb = [p.sb("h_Sb%d" % h, [128, (NCH + 1) * 128], BF16, st) for h in range(H)]
        p.dma("sp", nw[:, :], bc_rows(nw_dram))
        for h in range(H):
            p.memset("pool", Ss[h][:, :], 0.0)
            p.memset("pool", Sb[h][:, 0:128], 0.0)
        qo, fo, vo, go = FM_OFF["h_q"], FM_OFF["h_f"], TM_OFF["h_i"], TM_OFF["h_g"]
        for n in range(NT):
            i = n % 2
            t0 = n * 128
            p.dma("sp", qf[i][:, :, :], fm_s[qo:qo + 1024, t0:t0 + 128].rearrange("(h p) t -> p h t", p=128))
            p.dma("actq", ff[i][:, :, :], fm_s[fo:fo + 1024, t0:t0 + 128].rearrange("(h p) t -> p h t", p=128))
            p.dma("sp", vt[i][:, :], tm_s[t0:t0 + 128, vo:vo + 1024])
            p.dma("actq", gt[i][:, :], tm_s[t0:t0 + 128, go:go + 1024])
            p.act(sg[i][:, :, :], ff[i][:, :, :], AF.Sigmoid)
            p.act(qf[i][:, :, :], qf[i][:, :, :], AF.Silu)
            p.copy("pool", vb[i][:, :], vt[i][:, :])
            p.act(gt[i][:, :], gt[i][:, :], AF.Sigmoid)
            nw_bc = nw[:, :].rearrange("p (o v) -> p o v", o=1).to_broadcast([128, H, 128])
            p.tt("pool", gw[i][:, :, :], gt[i][:, :].rearrange("p (h v) -> p h v", v=128), nw_bc, ALU.mult)
            for h in range(H):
                j = h % 2
                sl = R.next()
                p.ts("dve", lg[j][:, :], sg[i][:, h, :], oml[:, h:h + 1], lb[:, h:h + 1], ALU.mult, ALU.add)
                p.act(lg[j][:, :], lg[j][:, :], AF.Ln)
                p.ts("dve", kk[j][:, :], sg[i][:, h, :], -1.0, 1.0, ALU.mult, ALU.add)
                p.ts("dve", kk[j][:, :], kk[j][:, :], oml[:, h:h + 1], None, ALU.mult)
                o = mix_core(p, R, sl, qf[i][:, h, :], kk[j][:, :], lg[j][:, :], vb[i][:, h * 128:(h + 1) * 128], 128, Ss[h], Sb[h])
                if o is None or DBG in (4, 5):
                    continue
                rstd = rstd_from_psum(p, o, 128, scr[j])
                p.stt(yb[j][:, :], o, rstd, gw[i][:, h, :], ALU.mult, ALU.mult)
                p.tr(sl["yT"], yb[j][:, :], R.C.B("ident"))
                p.copy("act", yT[i][:, h, :], sl["yT"])
            p.dma("sp", yT_s[1024:2048, t0:t0 + 128].rearrange("(h p) t -> p h t", p=128), yT[i][:, :, :])


def post_rms(p, R, sl, o, Vd, gw_h, scr, yb, yT_dst_list, eng_toggle=0):
    rstd = rstd_from_psum(p, o, Vd, scr)
    p.stt(yb[:, :Vd], o, rstd, gw_h, ALU.mult, ALU.mult)
    for j, dst in enumerate(yT_dst_list):
        p.tr(sl["yT"], yb[:, j * 128:(j + 1) * 128], R.C.B("ident"))
        p.copy("act", dst, sl["yT"])


def phase_gla(p, R, fm_s, tm_s, yT_s, gw_dram, gb_dram, nw_dram):
    H = 4
    with p.scope() as st:
        qf = [p.sb("g_qf%d" % i, [128, H, 128], F32, st) for i in range(2)]
        kf = [p.sb("g_kf%d" % i, [128, H, 128], F32, st) for i in range(2)]
        lr = [p.sb("g_lr%d" % i, [16, 128], F32, st) for i in range(2)]
        vt = [p.sb("g_vt%d" % i, [128, 1024], F32, st) for i in range(2)]
        gt = [p.sb("g_gt%d" % i, [128, 1024], F32, st) for i in range(2)]
        vb = [p.sb("g_vb%d" % i, [128, 1024], BF16, st) for i in range(2)]
        lg = [p.sb("g_lg%d" % i, [128, 128], F32, st) for i in range(2)]
        gwt = [p.sb("g_gw%d" % i, [128, H, 256], F32, st) for i in range(2)]
        scr = [p.sb("g_scr%d" % i, [128, 260], F32, st) for i in range(2)]
        yb = [p.sb("g_yb%d" % i, [128, 256], BF16, st) for i in range(2)]
        yT = [p.sb("g_yT%d" % i, [128, 2 * H, 128], BF16, st) for i in range(2)]
        nw = p.sb("g_nw", [128, 256], F32, st)
        gw = p.sb("g_gatew", [16, 512], F32, st)
        ngb = p.sb("g_ngb", [128, H], F32, st)
        Ss = [p.sb("g_S%d" % h, [128, 256], F32, st) for h in range(H)]
        Sb = [p.sb("g_Sb%d" % h, [128, (NCH + 1) * 256], BF16, st) for h in range(H)]
        p.dma("sp", nw[:, :], bc_rows(nw_dram))
        p.dma("sp", gw[:, :], gw_dram)
        p.dma("sp", ngb[:, :], gb_dram.rearrange("(h p) -> p h", p=128), allow_slow_non_contiguous=True)
        p.ts("dve", ngb[:, :], ngb[:, :], -1.0, None, ALU.mult)
        for h in range(H):
            p.memset("pool", Ss[h][:, :], 0.0)
            p.memset("pool", Sb[h][:, 0:256], 0.0)
        qo, ko, lo, vo, go = FM_OFF["g_q"], FM_OFF["g_k"], FM_OFF["g_lr"], TM_OFF["g_v"], TM_OFF["g_g"]
        for n in range(NT):
            i = n % 2
            t0 = n * 128
            p.dma("sp", qf[i][:, :, :], fm_s[qo:qo + 512, t0:t0 + 128].rearrange("(h p) t -> p h t", p=128))
            p.dma("actq", kf[i][:, :, :], fm_s[ko:ko + 512, t0:t0 + 128].rearrange("(h p) t -> p h t", p=128))
            p.dma("sp", lr[i][:, :], fm_s[lo:lo + 16, t0:t0 + 128])
            p.dma("sp", vt[i][:, :], tm_s[t0:t0 + 128, vo:vo + 1024])
            p.dma("actq", gt[i][:, :], tm_s[t0:t0 + 128, go:go + 1024])
            p.act(qf[i][:, :, :], qf[i][:, :, :], AF.Copy, scale=float(128 ** -0.5))
            p.copy("pool", vb[i][:, :], vt[i][:, :])
            p.act(gt[i][:, :], gt[i][:, :], AF.Silu)
            nw_bc = nw[:, :].rearrange("p (o v) -> p o v", o=1).to_broadcast([128, H, 256])
            p.tt("pool", gwt[i][:, :, :], gt[i][:, :].rearrange("p (h v) -> p h v", v=256), nw_bc, ALU.mult)
            for h in range(H):
                j = h % 2
                sl = R.next()
                p.mm(sl["A"], gw[:, h * 128:(h + 1) * 128], lr[i][:, :])
                p.act(lg[j][:, :], sl["A"], AF.Exp, bias=ngb[:, h:h + 1], scale=-1.0)
                p.act(lg[j][:, :], lg[j][:, :], AF.Ln, bias=1.0)
                p.ts("dve", lg[j][:, :], lg[j][:, :], -1.0 / 16.0, None, ALU.mult)
                o = mix_core(p, R, sl, qf[i][:, h, :], kf[i][:, h, :], lg[j][:, :], vb[i][:, h * 256:(h + 1) * 256], 256, Ss[h], Sb[h])
                post_rms(p, R, sl, o, 256, gwt[i][:, h, :], scr[j], yb[j], [yT[i][:, 2 * h, :], yT[i][:, 2 * h + 1, :]])
            p.dma("sp", yT_s[3072:4096, t0:t0 + 128].rearrange("(h p) t -> p h t", p=128), yT[i][:, :, :])


def conv_block(p, xin, cw, cb, nct, out_fn, first):
    for c in range(nct):
        dst = out_fn(c)
        eng = "dve"
        p.ts(eng, dst, xin[:, c, 3:131], cw[:, 3, c:c + 1], None, ALU.mult)
        for j in (2, 1, 0):
            p.stt(dst, xin[:, c, j:j + 128], cw[:, j, c:c + 1], dst, ALU.mult, ALU.add)
        if cb is not None:
            p.act(dst, dst, AF.Silu, bias=cb[:, c:c + 1])
        else:
            p.act(dst, dst, AF.Silu)


def bcast_logg(p, sl, col_view, C):
    dst = sl["A"]
    p.mm(dst, V(col_view.ap.to_broadcast([128, 128]), col_view.buf), C.F("ident"))
    return dst


def load_halo(p, q, dst, fm_s, r0, nrows, n):
    t0 = n * 128
    if n == 0:
        p.memset("pool", dst[:, :, 0:3], 0.0)
        p.dma(q, dst[:, :, 3:131], fm_s[r0:r0 + nrows, 0:128].rearrange("(c p) t -> p c t", p=128))
    else:
        p.dma(q, dst[:, :, :], fm_s[r0:r0 + nrows, t0 - 3:t0 + 128].rearrange("(c p) t -> p c t", p=128))


def phase_ssd(p, R, fm_s, tm_s, yT_s, prm, l):
    def L(name):
        t = prm[name]
        return t[l] if len(t.h.shape) > 1 and t.h.shape[0] == DEPTH and name not in () else t[:]
    C = R.C
    with p.scope() as st:
        xin = [p.sb("m_xin%d" % i, [128, 16, 131], F32, st) for i in range(2)]
        xc = [p.sb("m_xc%d" % i, [128, 16, 128], F32, st) for i in range(2)]
        zt = [p.sb("m_zt%d" % i, [128, 1024], F32, st) for i in range(2)]
        dtt = [p.sb("m_dt%d" % i, [128, 16], F32, st) for i in range(2)]
        dta = [p.sb("m_dta%d" % i, [128, 16], F32, st) for i in range(2)]
        xsT = [p.sb("m_xsT%d" % i, [128, 1024], F32, st) for i in range(2)]
        vb = [p.sb("m_vb%d" % i, [128, 1024], BF16, st) for i in range(2)]
        yg = [p.sb("m_yg%d" % i, [128, 256], F32, st) for i in range(2)]
        scr = [p.sb("m_scr%d" % i, [128, 260], F32, st) for i in range(2)]
        yb = [p.sb("m_yb%d" % i, [128, 256], BF16, st) for i in range(2)]
        yT = [p.sb("m_yT%d" % i, [128, 8, 128], BF16, st) for i in range(2)]
        cw = p.sb("m_cw", [128, 4, 16], F32, st)
        cb = p.sb("m_cb", [128, 16], F32, st)
        nw = p.sb("m_nw", [128, 1024], F32, st)
        rowp = p.sb("m_rowp", [128, 48], F32, st)
        Ss = [p.sb("m_S%d" % h, [128, 64], F32, st) for h in range(16)]
        Sb = [p.sb("m_Sb%d" % h, [128, (NCH + 1) * 64], BF16, st) for h in range(16)]
        for j in range(4):
            p.dma("sp", cw[:, j, :], prm["m_conv_w"][l, j].rearrange("(c p) -> p c", p=128), allow_slow_non_contiguous=True)
        p.dma("sp", cb[:, :], prm["m_conv_b"][l].rearrange("(c p) -> p c", p=128), allow_slow_non_contiguous=True)
        p.dma("sp", nw[:, :], bc_rows(prm["m_norm_w"][l:l + 1, :]))
        p.dma("sp", rowp[:, 0:16], bc_rows(prm["m_dt_bias"][l:l + 1, :]))
        p.dma("sp", rowp[:, 16:32], bc_rows(prm["m_a_log"][l:l + 1, :]))
        p.dma("sp", rowp[:, 32:48], bc_rows(prm["m_d_skip"][l:l + 1, :]))
        p.act(rowp[:, 16:32], rowp[:, 16:32], AF.Exp)
        p.ts("dve", rowp[:, 16:32], rowp[:, 16:32], -1.0, None, ALU.mult)
        for h in range(16):
            p.memset("pool", Ss[h][:, :], 0.0)
            p.memset("pool", Sb[h][:, 0:64], 0.0)
        xo, zo, do = FM_OFF["m_xbc"], TM_OFF["m_z"], TM_OFF["m_dt"]
        for n in range(NT):
            i = n % 2
            t0 = n * 128
            load_halo(p, "sp", xin[i], fm_s, xo, 2048, n)
            p.dma("actq", zt[i][:, :], tm_s[t0:t0 + 128, zo:zo + 1024])
            p.dma("actq", dtt[i][:, :], tm_s[t0:t0 + 128, do:do + 16])
            conv_block(p, xin[i], cw, cb, 16, lambda c: xc[i][:, c, :], n == 0)
            p.tt("dve", dtt[i][:, :], dtt[i][:, :], rowp[:, 0:16], ALU.add)
            p.act(dtt[i][:, :], dtt[i][:, :], AF.Exp)
            p.act(dtt[i][:, :], dtt[i][:, :], AF.Ln, bias=1.0)
            p.tt("dve", dta[i][:, :], dtt[i][:, :], rowp[:, 16:32], ALU.mult)
            p.act(zt[i][:, :], zt[i][:, :], AF.Silu)
            for c in range(8):
                sl = R.next()
                p.tr(sl["ft"], xc[i][:, c, :], C.F("ident"))
                p.copy("act", xsT[i][:, c * 128:(c + 1) * 128], sl["ft"])
            dt_bc = dtt[i][:, :].rearrange("p (h o) -> p h o", o=1).to_broadcast([128, 16, 64])
            p.tt("dve", vb[i][:, :].rearrange("p (h d) -> p h d", d=64), xsT[i][:, :].rearrange("p (h d) -> p h d", d=64), dt_bc, ALU.mult)
            for g in range(4):
                gi = g % 2
                for hh in range(4):
                    h = g * 4 + hh
                    sl = R.next()
                    logg = bcast_logg(p, sl, dta[i][:, h:h + 1], C)
                    o = mix_core(p, R, sl, xc[i][:, 12 + g, :], xc[i][:, 8 + g, :], logg, vb[i][:, h * 64:(h + 1) * 64], 64, Ss[h], Sb[h])
                    p.stt(yg[gi][:, hh * 64:(hh + 1) * 64], xsT[i][:, h * 64:(h + 1) * 64], rowp[:, 32 + h:33 + h], o, ALU.mult, ALU.add)
                p.tt("dve", yg[gi][:, :], yg[gi][:, :], zt[i][:, g * 256:(g + 1) * 256], ALU.mult)
                sc = scr[gi]
                p.act(sc[:, :256], yg[gi][:, :], AF.Square, accum_out=sc[:, 256:257])
                p.ts("dve", sc[:, 257:258], sc[:, 256:257], 1.0 / 256, 1e-5, ALU.mult, ALU.add)
                p.act(sc[:, 257:258], sc[:, 257:258], AF.Sqrt)
                p.recip(sc[:, 258:259], sc[:, 257:258])
                p.stt(yb[gi][:, :], yg[gi][:, :], sc[:, 258:259], nw[:, g * 256:(g + 1) * 256], ALU.mult, ALU.mult)
                for j in range(2):
                    sl = R.next()
                    p.tr(sl["yT"], yb[gi][:, j * 128:(j + 1) * 128], C.B("ident"))
                    p.copy("act", yT[i][:, g * 2 + j, :], sl["yT"])
            p.dma("sp", yT_s[0:1024, t0:t0 + 128].rearrange("(h p) t -> p h t", p=128), yT[i][:, :, :])


def phase_gdn(p, R, fm_s, tm_s, yT_s, prm, l):
    C = R.C
    H = 8
    with p.scope() as st:
        xin = [p.sb("d_xin0", [128, 24, 131], F32, st)] * 2
        xc = [p.sb("d_xc0", [128, 24, 128], F32, st)] * 2
        gt = [p.sb("d_gt%d" % i, [128, 1024], F32, st) for i in range(2)]
        ab = [p.sb("d_ab%d" % i, [128, 16], F32, st) for i in range(2)]
        la = [p.sb("d_la%d" % i, [128, 8], F32, st) for i in range(2)]
        gw = [p.sb("d_gw%d" % i, [128, H, 128], F32, st) for i in range(2)]
        vf = [p.sb("d_vf%d" % i, [128, 128], F32, st) for i in range(2)]
        sq = [p.sb("d_sq%d" % i, [128, 128], F32, st) for i in range(2)]
        scr = [p.sb("d_scr%d" % i, [128, 132], F32, st) for i in range(2)]
        yb = [p.sb("d_yb%d" % i, [128, 128], BF16, st) for i in range(2)]
        yT = [p.sb("d_yT%d" % i, [128, H, 128], BF16, st) for i in range(2)]
        G = []
        for i in range(2):
            d = {}
            for nm in ("N", "NT", "N2", "N2T", "N4", "N4T", "N8T"):
                d[nm] = p.sb("d_%s%d" % (nm, i), [128, 128], F32, st)
            d["kdec"] = p.sb("d_kdec%d" % i, [128, 128], BF16, st)
            d["kcb"] = p.sb("d_kcb%d" % i, [128, 128], BF16, st)
            d["X"] = [p.sb("d_X%d_%d" % (i, j), [128, 256], F32, st) for j in range(2)]
            d["kcm"] = p.sb("d_kcm%d" % i, [128, NCH * 128], BF16, st)
            d["atm"] = p.sb("d_atm%d" % i, [128, NCH * 128], BF16, st)
            d["vn"] = p.sb("d_vn%d" % i, [128, NCH * 128], BF16, st)
            G.append(d)
        cw = p.sb("d_cw", [128, 4, 24], F32, st)
        nw = p.sb("d_nw", [128, 128], F32, st)
        rowp = p.sb("d_rowp", [128, 16], F32, st)
        Ss = [p.sb("d_S%d" % h, [128, 128], F32, st) for h in range(H)]
        Sb = [p.sb("d_Sb%d" % h, [128, (NCH + 1) * 128], BF16, st) for h in range(H)]
        for j in range(4):
            p.dma("sp", cw[:, j, :], prm["dn_conv_w"][l, j].rearrange("(c p) -> p c", p=128), allow_slow_non_contiguous=True)
        p.dma("sp", nw[:, :], bc_rows(prm["dn_norm_w"][l:l + 1, :]))
        p.dma("sp", rowp[:, 0:8], bc_rows(prm["dn_a_log"][l:l + 1, :]))
        p.dma("sp", rowp[:, 8:16], bc_rows(prm["dn_dt_bias"][l:l + 1, :]))
        p.act(rowp[:, 0:8], rowp[:, 0:8], AF.Exp)
        p.ts("dve", rowp[:, 0:8], rowp[:, 0:8], -1.0, None, ALU.mult)
        for h in range(H):
            p.memset("pool", Ss[h][:, :], 0.0)
            p.memset("pool", Sb[h][:, 0:128], 0.0)
        xo, go, ao, bo = FM_OFF["d_q"], TM_OFF["d_g"], TM_OFF["d_a"], TM_OFF["d_b"]
        assert bo == ao + 8
        for n in range(NT):
            i = n % 2
            t0 = n * 128
            load_halo(p, "sp", xin[i], fm_s, xo, 3072, n)
            p.dma("actq", gt[i][:, :], tm_s[t0:t0 + 128, go:go + 1024])
            p.dma("actq", ab[i][:, :], tm_s[t0:t0 + 128, ao:ao + 16])
            conv_block(p, xin[i], cw, None, 24, lambda c: xc[i][:, c, :], n == 0)
            p.tt("dve", la[i][:, :], ab[i][:, 0:8], rowp[:, 8:16], ALU.add)
            p.act(la[i][:, :], la[i][:, :], AF.Exp)
            p.act(la[i][:, :], la[i][:, :], AF.Ln, bias=1.0)
            p.tt("dve", la[i][:, :], la[i][:, :], rowp[:, 0:8], ALU.mult)
            p.act(ab[i][:, 8:16], ab[i][:, 8:16], AF.Sigmoid)
            p.act(gt[i][:, :], gt[i][:, :], AF.Silu)
            nw_bc = nw[:, :].rearrange("p (o v) -> p o v", o=1).to_broadcast([128, H, 128])
            p.tt("pool", gw[i][:, :, :], gt[i][:, :].rearrange("p (h v) -> p h v", v=128), nw_bc, ALU.mult)
            for h in range(H):
                j = h % 2
                sl = R.next()
                for (c, scale) in ((h, float(128 ** -0.5)), (8 + h, 1.0)):
                    p.act(sq[j][:, :], xc[i][:, c, :], AF.Square)
                    p.mm(sl["A"], C.F("ones"), sq[j][:, :])
                    p.act(sq[j][:, :], sl["A"], AF.Sqrt, bias=1e-6)
                    p.recip(sq[j][:, :], sq[j][:, :])
                    p.stt(xc[i][:, c, :], xc[i][:, c, :], scale, sq[j][:, :], ALU.mult, ALU.mult)
                p.tr(sl["ft"], xc[i][:, 16 + h, :], C.F("ident"))
                p.copy("act", vf[j][:, :], sl["ft"])
                logg = bcast_logg(p, sl, la[i][:, h:h + 1], C)
                g = dict(G[j])
                g["beta"] = ab[i][:, 8 + h:9 + h]
                g["vf"] = vf[j][:, :]
                o = mix_core(p, R, sl, xc[i][:, h, :], xc[i][:, 8 + h, :], logg, None, 128, Ss[h], Sb[h], gdn=g)
                if o is None:
                    continue
                post_rms(p, R, sl, o, 128, gw[i][:, h, :], scr[j], yb[j], [yT[i][:, h, :]])
            p.dma("sp", yT_s[2048:3072, t0:t0 + 128].rearrange("(h p) t -> p h t", p=128), yT[i][:, :, :])


PARAM_SHAPES = {"m_conv_w": [4, 2048], "m_conv_b": [2048], "m_dt_bias": [16], "m_a_log": [16], "m_d_skip": [16],
                "m_norm_w": [1024], "h_lb_logits": [1024], "h_norm_w": [128], "dn_conv_w": [4, 3072], "dn_a_log": [8],
                "dn_dt_bias": [8], "dn_norm_w": [128], "g_gate_w": [16, 512], "g_gate_b": [512], "g_norm_w": [256],
                "w_branch": [4, 1024, 2048], "w_merge": [2048, 8192], "b_merge": [8192], "w_out": [2048, 2048],
                "ln1_w": [2048], "ln1_b": [2048], "p_w_query": [2048, 2048], "ln2_w": [2048], "ln2_b": [2048]}


def compute_lb(p, lbt, logits_d, n_layers):
    with p.scope() as st:
        e = p.sb("lb_e", [128, DEPTH, 8], F32, st)
        sm = p.sb("lb_sm", [128, 8], F32, st)
        for l in range(DEPTH):
            p.dma("sp", e[:, l, :], logits_d[l].rearrange("(h p) -> p h", p=128), allow_slow_non_contiguous=True)
        p.act(e[:, :, :], e[:, :, :], AF.Exp)
        p.tt("dve", sm[:, :], e[:, 0, :], e[:, 1, :], ALU.add)
        for l in range(2, DEPTH):
            p.tt("dve", sm[:, :], sm[:, :], e[:, l, :], ALU.add)
        p.recip(sm[:, :], sm[:, :])
        for l in range(DEPTH):
            p.tt("dve", e[:, l, :], e[:, l, :], sm[:, :], ALU.mult)
        for l in range(n_layers):
            if l == 0:
                p.memset("dve", lbt[:, l, 0:8], 0.0)
            elif l == 1:
                p.copy("dve", lbt[:, l, 0:8], e[:, 1, :])
            else:
                p.tt("dve", lbt[:, l, 0:8], lbt[:, l - 1, 0:8], e[:, l, :], ALU.add)
            p.ts("dve", lbt[:, l, 8:16], lbt[:, l, 0:8], -1.0, 1.0, ALU.mult, ALU.add)


def build_program(n_layers=DEPTH):
    nc = bass.Bass("TRN2", target_bir_lowering=False)
    with ExitStack() as es:
        p = Prog(nc, es)
        x = p.dram("x", [S, D], F32, kind="ExternalInput")
        w_in = p.dram("w_in", [DEPTH, D, FM_W + TM_W], F32, kind="ExternalInput")
        cd = p.dram("cd", list(host_const_array().shape), F32, kind="ExternalInput")
        prm = {}
        for k, shp in PARAM_SHAPES.items():
            prm[k] = p.dram(k, [DEPTH] + shp, F32, kind="ExternalInput")
        skT = p.dram("p_skT", [DEPTH, 2, 128, 128], F32, kind="ExternalInput")
        euT = p.dram("p_euT", [DEPTH, D, 16384], F32, kind="ExternalInput")
        ev = p.dram("p_expert_v", [DEPTH, 16384, D], F32, kind="ExternalInput")
        out = p.dram("out", [S, D], F32, kind="ExternalOutput")
        fm_s = p.dram("fm_s", [FM_W, S], F32)
        tm_s = p.dram("tm_s", [S, TM_W], F32)
        yT_s = p.dram("yT_s", [4096, S], BF16)
        mT_s = p.dram("mT_s", [D, S], BF16)
        x1_s = p.dram("x1_s", [S, D], F32)
        xb = [p.dram("xb%d_s" % i, [S, D], F32) for i in range(2)]
        C = Consts(p, cd)
        lbt = p.sb("lbt", [128, DEPTH, 16], F32)
        compute_lb(p, lbt, prm["h_lb_logits"], n_layers)
        for l in range(n_layers):
            x_in = x if l == 0 else xb[(l - 1) % 2]
            x_out = out if l == n_layers - 1 else xb[l % 2]
            with p.scope() as sx:
                xT = p.sb("xT", [128, KC, S], BF16, sx)
                phase_xT(p, x_in, xT, C.F("ident"))
                phase_proj(p, xT, w_in[l], fm_s, tm_s)
                with p.scope() as st:
                    R = MixRes(p, st, C)
                    phase_ssd(p, R, fm_s, tm_s, yT_s, prm, l)
                    phase_hgrn2(p, R, fm_s, tm_s, yT_s, lbt[:, l, 0:8], lbt[:, l, 8:16], prm["h_norm_w"][l:l + 1, :])
                    phase_gdn(p, R, fm_s, tm_s, yT_s, prm, l)
                    phase_gla(p, R, fm_s, tm_s, yT_s, prm["g_gate_w"][l], prm["g_gate_b"][l], prm["g_norm_w"][l:l + 1, :])
                phase_merge(p, xT, yT_s, mT_s, x_in, x1_s, prm["w_branch"][l], prm["w_merge"][l], prm["b_merge"][l],
                            prm["w_out"][l], prm["ln1_w"][l:l + 1, :], prm["ln1_b"][l:l + 1, :])
            phase_peer(p, C, x1_s, x_out, prm["p_w_query"][l], skT[l], euT[l], ev[l],
                       prm["ln2_w"][l:l + 1, :], prm["ln2_b"][l:l + 1, :])
        for k_, (s_, v_) in out.buf.pend_w.items():
            p.finish_output(s_, v_)
        p.n_total = p.n_inst
        p.emit()
    return nc


_NC_CACHE = {}


def kernel(**inputs):
    f32 = lambda a: np.ascontiguousarray(np.asarray(a, dtype=np.float32))
    x = f32(inputs["x"])
    base = {"cd": host_const_array()}
    base["w_in"] = np.ascontiguousarray(np.asarray(inputs["w_in"], dtype=np.float32)[:, :, W_IN_PERM])
    for k in PARAM_SHAPES:
        base[k] = f32(inputs[k])
    base["p_skT"] = np.ascontiguousarray(np.asarray(inputs["p_sub_keys"], dtype=np.float32).transpose(0, 1, 3, 2))
    base["p_euT"] = np.ascontiguousarray(np.asarray(inputs["p_expert_u"], dtype=np.float32).transpose(0, 2, 1))
    base["p_expert_v"] = f32(inputs["p_expert_v"])
    if "nc" not in _NC_CACHE:
        _NC_CACHE["nc"] = build_program(DEPTH)
    nc = _NC_CACHE["nc"]
    in_maps = []
    for c in range(8):
        m = dict(base)
        m["x"] = np.ascontiguousarray(x[c % 4])
        in_maps.append(m)
    res = run_bass_kernel_spmd(nc, in_maps, core_ids=list(range(8)))
    return np.stack([np.asarray(res.results[b]["out"], dtype=np.float32) for b in range(4)], axis=0)


ALPHA = float((2 * DEPTH) ** 0.25)


def layer_norm_tile(p, h, junk, gain, bias, scr, eps=1e-5):
    p.op("dve", lambda e, o=scr[:, 0:1].ap, i=h.ap: e.reduce_sum(o, i, AX.X), reads=[h], writes=[scr])
    p.ts("dve", scr[:, 1:2], scr[:, 0:1], -1.0 / D, None, ALU.mult)
    p.ts("dve", h, h, scr[:, 1:2], None, ALU.add)
    p.act(junk, h, AF.Square, accum_out=scr[:, 2:3])
    p.ts("dve", scr[:, 3:4], scr[:, 2:3], 1.0 / D, eps, ALU.mult, ALU.add)
    p.act(scr[:, 3:4], scr[:, 3:4], AF.Sqrt)
    p.recip(scr[:, 4:5], scr[:, 3:4])
    p.stt(h, h, scr[:, 4:5], gain, ALU.mult, ALU.mult)
    p.tt("pool", h, h, bias, ALU.add)


def phase_merge(p, xT, yT_s, mT_s, x_s, x1_s, wbr_d, wmg_d, bmg_d, wout_d, g_d, b_d):
    with p.scope() as st:
        yts = p.sb("mg_yT", [128, 32, 512], BF16, st)
        wbf = [p.sb("mg_wbf%d" % i, [128, 2, 8, 128], F32, st) for i in range(2)]
        wmf = [p.sb("mg_wmf%d" % i, [128, 16, 2, 128], F32, st) for i in range(2)]
        wbb = [p.sb("mg_wbb%d" % i, [128, 2, 8, 128], BF16, st) for i in range(2)]
        wmb = [p.sb("mg_wmb%d" % i, [128, 16, 2, 128], BF16, st) for i in range(2)]
        bm = p.sb("mg_bm", [128, 4, 16], F32, st)
        gs = [p.sb("mg_g%d" % i, [128, 512], F32, st) for i in range(2)]
        acc = [p.sb("mg_acc%d" % i, [128, 512], F32, st) for i in range(2)]
        mo = [p.sb("mg_mo%d" % i, [128, 512], BF16, st) for i in range(2)]
        pbd = [p.ps("mg_pbd%d" % i, [128, 512], F32, st) for i in range(2)]
        pgt = [p.ps("mg_pgt%d" % i, [128, 512], F32, st) for i in range(2)]
        for n in range(4):
            p.dma("sp", bm[:, n, :], bmg_d[n * 2048:(n + 1) * 2048].rearrange("(c p) -> p c", p=128), allow_slow_non_contiguous=True)
        it = 0
        for ts in range(4):
            tsl = slice(ts * 512, (ts + 1) * 512)
            p.dma("sp", yts[:, :, :], yT_s[:, tsl].rearrange("(c p) t -> p c t", p=128))
            for dc in range(16):
                dsl = slice(dc * 128, (dc + 1) * 128)
                a = acc[dc % 2]
                for half in range(2):
                    i = it % 2
                    it += 1
                    for nn in range(2):
                        n = half * 2 + nn
                        p.dma("sp", wbf[i][:, nn, :, :], wbr_d[n, :, dsl].rearrange("(c p) d -> p c d", p=128))
                        p.dma("actq", wmf[i][:, :, nn, :], wmg_d[:, n * 2048 + dc * 128:n * 2048 + (dc + 1) * 128].rearrange("(k p) d -> p k d", p=128))
                    p.copy("pool", wbb[i][:, :, :, :], wbf[i][:, :, :, :])
                    p.copy("pool", wmb[i][:, :, :, :], wmf[i][:, :, :, :])
                    for nn in range(2):
                        n = half * 2 + nn
                        pb, pg = pbd[n % 2], pgt[n % 2]
                        for c in range(8):
                            p.mm(pb[:, :], wbb[i][:, nn, c, :], yts[:, n * 8 + c, :], start=(c == 0), stop=(c == 7))
                        for k in range(KC):
                            p.mm(pg[:, :], wmb[i][:, k, nn, :], xT[:, k, tsl], start=(k == 0), stop=(k == KC - 1))
                        g = gs[n % 2]
                        p.act(g[:, :], pg[:, :], AF.Sigmoid, bias=bm[:, n, dc:dc + 1])
                        if n == 0:
                            p.tt("dve", a[:, :], g[:, :], pb[:, :], ALU.mult)
                        else:
                            p.tt("dve", g[:, :], g[:, :], pb[:, :], ALU.mult)
                            p.tt("pool", a[:, :], a[:, :], g[:, :], ALU.add)
                m = mo[dc % 2]
                p.copy("act", m[:, :], a[:, :])
                p.dma("sp", mT_s[dsl, tsl], m[:, :])
    with p.scope() as st:
        wo = p.sb("mg_wo", [128, KC, 2048], BF16, st)
        wof = [p.sb("mg_wof%d" % i, [128, KC, 128], F32, st) for i in range(2)]
        mt = [p.sb("mg_mt%d" % i, [128, KC, 128], BF16, st) for i in range(2)]
        xt = [p.sb("mg_x0", [128, D], F32, st)] * 2
        junk = p.sb("mg_junk", [128, D], BF16, st)
        gb = p.sb("mg_gb", [128, 2, D], F32, st)
        scr = [p.sb("mg_scr%d" % i, [128, 8], F32, st) for i in range(2)]
        po = [p.ps("mg_po%d" % i, [128, 512], F32, st) for i in range(8)]
        p.dma("sp", gb[:, 0, :], bc_rows(g_d))
        p.dma("sp", gb[:, 1, :], bc_rows(b_d))
        for j in range(16):
            wf = wof[j % 2]
            p.dma("sp" if j % 2 == 0 else "actq", wf[:, :, :], wout_d[:, j * 128:(j + 1) * 128].rearrange("(k p) c -> p k c", p=128))
            p.copy("pool", wo[:, :, j * 128:(j + 1) * 128], wf[:, :, :])
        for t in range(NT):
            i = t % 2
            tsl = slice(t * 128, (t + 1) * 128)
            p.dma("sp", mt[i][:, :, :], mT_s[:, tsl].rearrange("(k p) t -> p k t", p=128))
            p.dma("actq", xt[i][:, :], x_s[tsl, :])
            for cg in range(4):
                ps = po[(t % 2) * 4 + cg]
                for k in range(KC):
                    p.mm(ps[:, :], mt[i][:, k, :], wo[:, k, cg * 512:(cg + 1) * 512], start=(k == 0), stop=(k == KC - 1))
                p.stt(xt[i][:, cg * 512:(cg + 1) * 512], xt[i][:, cg * 512:(cg + 1) * 512], ALPHA, ps[:, :], ALU.mult, ALU.add)
            layer_norm_tile(p, xt[i][:, :], junk[:, :], gb[:, 0, :], gb[:, 1, :], scr[i])
            p.dma("sp", x1_s[tsl, :], xt[i][:, :])


GELU_C = 1.5957691216057308
EG = 4


def phase_peer(p, C, x1_s, x2_s, wq_d, skT_d, UT_d, V_d, g_d, b_d):
    with p.scope() as st:
        x1T = p.sb("pe_x1T", [128, KC, 512], BF16, st)
        acc = p.sb("pe_acc", [128, 4, D], F32, st)
        sel = [p.sb("pe_sel%d" % i, [128, 16, 128], F32, st) for i in range(4)]
        tpr = [p.sb("pe_tpr%d" % i, [128, 8, 128], F32, st) for i in range(4)]
        skT = p.sb("pe_skT", [128, 2, 128], F32, st)
        gb = p.sb("pe_gb", [128, 2, D], F32, st)
        p.dma("sp", skT[:, :, :], skT_d.rearrange("h c n -> c h n"))
        p.dma("sp", gb[:, 0, :], bc_rows(g_d))
        p.dma("sp", gb[:, 1, :], bc_rows(b_d))
        for ts in range(4):
            with p.scope() as s2:
                xin = [p.sb("pe_xin%d" % i, [128, D], F32, s2) for i in range(2)]
                wqf = [p.sb("pe_wqf%d" % i, [128, KC, 128], F32, s2) for i in range(2)]
                wqb = [p.sb("pe_wqb%d" % i, [128, KC, 128], BF16, s2) for i in range(2)]
                qr = [p.sb("pe_qr%d" % i, [128, 512], F32, s2) for i in range(2)]
                scb = [p.sb("pe_scb%d" % i, [128, 128], F32, s2) for i in range(2)]
                scc = [p.sb("pe_scc%d" % i, [128, 128], F32, s2) for i in range(2)]
                tv = [p.sb("pe_tv%d" % i, [128, 16, 16], F32, s2) for i in range(4)]
                nm = [p.sb("pe_nm%d" % i, [128, 1], F32, s2) for i in range(2)]
                cand = p.sb("pe_cand", [128, 8, 256], F32, s2)
                cnd2 = p.sb("pe_cnd2", [128, 256], F32, s2)
                bv = p.sb("pe_bv", [128, 8, 16], F32, s2)
                sm = p.sb("pe_sm", [128, 64], F32, s2)
                rea = p.sb("pe_rea", [128, 8, 128], F32, s2)
                pst = [p.ps("pe_pst%d" % i, [128, 4, 128], F32, s2) for i in range(2)]
                pq = [p.ps("pe_pq%d" % i, [128, 512], F32, s2) for i in range(2)]
                psc = [p.ps("pe_psc%d" % i, [128, 128], F32, s2) for i in range(2)]
                k2 = 0
                for tt in range(4):
                    t = ts * 4 + tt
                    xi = xin[tt % 2]
                    p.dma("sp", xi[:, :], x1_s[t * 128:(t + 1) * 128, :])
                    for g in range(KC // 4):
                        pt = pst[k2 % 2]
                        for j in range(4):
                            c = g * 4 + j
                            p.tr(pt[:, j, :], xi[:, c * 128:(c + 1) * 128], C.F("ident"))
                        p.copy("dve" if k2 % 2 == 0 else "act", x1T[:, g * 4:(g + 1) * 4, tt * 128:(tt + 1) * 128], pt[:, :, :])
                        k2 += 1
                for cc in range(16):
                    i = cc % 2
                    p.dma("sp" if i == 0 else "actq", wqf[i][:, :, :], wq_d[:, cc * 128:(cc + 1) * 128].rearrange("(k p) c -> p k c", p=128))
                    p.copy("pool", wqb[i][:, :, :], wqf[i][:, :, :])
                    for k in range(KC):
                        p.mm(pq[i][:, :], wqb[i][:, k, :], x1T[:, k, :], start=(k == 0), stop=(k == KC - 1))
                    p.copy("act", qr[i][:, :], pq[i][:, :])
                    for tt in range(4):
                        j = tt % 2
                        p.mm(psc[j][:, :], qr[i][:, tt * 128:(tt + 1) * 128], skT[:, cc % 2, :])
                        p.copy("act", scb[j][:, :], psc[j][:, :])
                        p.op("dve", lambda e, o=tv[tt][:, cc, 0:8].ap, a=scb[j][:, :].ap: e.max(o, a), reads=[scb[j]], writes=[tv[tt]])
                        p.op("dve", lambda e, o=scc[j][:, :].ap, r=tv[tt][:, cc, 0:8].ap, a=scb[j][:, :].ap: e.match_replace(o, r, a, -1e30),
                             reads=[scb[j], tv[tt]], writes=[scc[j]])
                        p.op("dve", lambda e, o=tv[tt][:, cc, 8:16].ap, a=scc[j][:, :].ap: e.max(o, a), reads=[scc[j]], writes=[tv[tt]])
                        p.ts("dve", nm[j][:, :], tv[tt][:, cc, 0:1], -1.0, None, ALU.mult)
                        p.act(sel[tt][:, cc, :], scb[j][:, :], AF.Exp, bias=nm[j][:, 0:1])
                for tt in range(4):
                    tv4 = tv[tt][:, :, :].rearrange("p (h two) k -> p h two k", two=2)
                    a0 = tv4[:, :, 0, :].rearrange("p h (k o) -> p h k o", o=1).to_broadcast([128, 8, 16, 16])
                    a1 = tv4[:, :, 1, :].rearrange("p h (o l) -> p h o l", o=1).to_broadcast([128, 8, 16, 16])
                    p.tt("dve", cand[:, :, :].rearrange("p h (k l) -> p h k l", l=16), a0, a1, ALU.add)
                    for h in range(8):
                        p.op("dve", lambda e, o=bv[:, h, 0:8].ap, a=cand[:, h, :].ap: e.max(o, a), reads=[cand], writes=[bv])
                        p.op("dve", lambda e, o=cnd2[:, :].ap, r=bv[:, h, 0:8].ap, a=cand[:, h, :].ap: e.match_replace(o, r, a, -1e30),
                             reads=[cand, bv], writes=[cnd2])
                        p.op("dve", lambda e, o=bv[:, h, 8:16].ap, a=cnd2[:, :].ap: e.max(o, a), reads=[cnd2], writes=[bv])
                    mx_bc = bv[:, :, 0:1].to_broadcast([128, 8, 16])
                    p.tt("dve", cand[:, :, 0:16], bv[:, :, :], mx_bc, ALU.subtract)
                    p.act(cand[:, :, 16:32], cand[:, :, 0:16], AF.Exp)
                    p.op("dve", lambda e, o=sm[:, 8:16].ap, a=cand[:, :, 16:32].ap: e.reduce_sum(o, a, AX.X), reads=[cand], writes=[sm])
                    p.recip(sm[:, 16:24], sm[:, 8:16])
                    p.act(sm[:, 24:32], cand[:, :, 15:16].rearrange("p h o -> p (h o)"), AF.Exp)
                    p.ts("dve", sm[:, 24:32], sm[:, 24:32], 1.0 - 1e-5, None, ALU.mult)
                    p.tt("dve", sm[:, 24:32], sm[:, 24:32], sm[:, 16:24], ALU.mult)
                    s4 = sel[tt][:, :, :].rearrange("p (h two) n -> p h two n", two=2)
                    rz_bc = sm[:, 16:24].rearrange("p (h o) -> p h o", o=1).to_broadcast([128, 8, 128])
                    p.tt("dve", s4[:, :, 1, :], s4[:, :, 1, :], rz_bc, ALU.mult)
                    p.recip(rea[:, :, :], s4[:, :, 0, :])
                    tz_bc = sm[:, 24:32].rearrange("p (h o) -> p h o", o=1).to_broadcast([128, 8, 128])
                    p.tt("dve", tpr[tt][:, :, :], rea[:, :, :], tz_bc, ALU.mult)
            with p.scope() as s2:
                uf = [p.sb("pe_uf%d" % i, [128, KC, 128], F32, s2) for i in range(2)]
                ub = [p.sb("pe_ub%d" % i, [128, KC, 128], BF16, s2) for i in range(2)]
                vf = [p.sb("pe_vf%d" % i, [128, D], F32, s2) for i in range(2)]
                vb = [p.sb("pe_vb%d" % i, [128, D], BF16, s2) for i in range(EG)]
                PT = [p.sb("pe_PT%d" % i, [128, 512], BF16, s2) for i in range(EG)]
                x2 = [p.sb("pe_x2%d" % i, [128, 512], F32, s2) for i in range(2)]
                gu = [p.sb("pe_gu%d" % i, [128, 512], F32, s2) for i in range(2)]
                wm = [p.sb("pe_wm%d" % i, [128, 128], F32, s2) for i in range(2)]
                wb = [p.sb("pe_wb%d" % i, [128, 128], BF16, s2) for i in range(2)]
                xt = vf[0]
                junk = vb[0]
                scr = p.sb("pe_scr", [128, 8], F32, s2)
                ph = [p.ps("pe_ph%d" % i, [128, 512], F32, s2) for i in range(2)]
                pw = [p.ps("pe_pw%d" % i, [128, 512], F32, s2) for i in range(2)]
                po = [p.ps("pe_po%d" % i, [128, 512], F32, s2) for i in range(4)]
                wi = 0
                for eg in range(128 // EG):
                    for e4 in range(EG):
                        ec = eg * EG + e4
                        i = ec % 2
                        p.dma("sp", uf[i][:, :, :], UT_d[:, ec * 128:(ec + 1) * 128].rearrange("(k p) e -> p k e", p=128))
                        p.dma("actq", vf[i][:, :], V_d[ec * 128:(ec + 1) * 128, :])
                        p.copy("pool", ub[i][:, :, :], uf[i][:, :, :])
                        p.copy("pool", vb[e4][:, :], vf[i][:, :])
                        for k in range(KC):
                            p.mm(ph[i][:, :], ub[i][:, k, :], x1T[:, k, :], start=(k == 0), stop=(k == KC - 1))
                        p.act(x2[i][:, :], ph[i][:, :], AF.Square)
                        p.ts("pool", x2[i][:, :], x2[i][:, :], 0.044715, 1.0, ALU.mult, ALU.add)
                        p.tt("dve", x2[i][:, :], x2[i][:, :], ph[i][:, :], ALU.mult)
                        p.act(x2[i][:, :], x2[i][:, :], AF.Sigmoid, scale=GELU_C)
                        p.tt("dve", gu[i][:, :], x2[i][:, :], ph[i][:, :], ALU.mult)
                        for tt in range(4):
                            s4 = sel[tt][:, :, :].rearrange("p (h two) n -> p h two n", two=2)
                            for h in range(8):
                                w2 = wi % 2
                                wi += 1
                                p.stt(wm[w2][:, :], s4[:, h, 1, :], tpr[tt][:, h, ec:ec + 1], s4[:, h, 1, :], ALU.is_ge, ALU.mult)
                                p.act(wb[w2][:, :], wm[w2][:, :], AF.Copy, scale=s4[:, h, 0, ec:ec + 1])
                                p.mm(pw[i][:, tt * 128:(tt + 1) * 128], wb[w2][:, :], C.B("ident"), start=(h == 0), stop=(h == 7))
                        p.tt("dve", PT[e4][:, :], gu[i][:, :], pw[i][:, :], ALU.mult)
                    for tt in range(4):
                        for dg in range(4):
                            for e4 in range(EG):
                                p.mm(po[dg][:, :], PT[e4][:, tt * 128:(tt + 1) * 128], vb[e4][:, dg * 512:(dg + 1) * 512],
                                     start=(e4 == 0), stop=(e4 == EG - 1))
                            a = acc[:, tt, dg * 512:(dg + 1) * 512]
                            if eg == 0:
                                p.copy("dve", a, po[dg][:, :])
                            else:
                                p.tt("dve", a, a, po[dg][:, :], ALU.add)
                for tt in range(4):
                    t = ts * 4 + tt
                    p.dma("sp", xt[:, :], x1_s[t * 128:(t + 1) * 128, :])
                    p.stt(xt[:, :], xt[:, :], ALPHA, acc[:, tt, :], ALU.mult, ALU.add)
                    layer_norm_tile(p, xt[:, :], junk[:, :], gb[:, 0, :], gb[:, 1, :], scr)
                    p.dma("sp", x2_s[t * 128:(t + 1) * 128, :], xt[:, :])
```
